# Optimizing a Trainium2 kernel written in Bass

```python
import jax, jax.numpy as jnp
from jax import lax
import numpy as np

D_MODEL = 2048
BATCH = 2
SEQ = 8192
DEPTH = 2

D_MIX = D_MODEL
D_SGU = D_MIX // 2
D_NA = D_MIX - D_SGU
SGU_CHUNK = 128
SGU_GROUP = 128
SGU_GROUPS = D_SGU // SGU_GROUP
NA_HEAD_DIM = 64
NA_HEADS = D_NA // NA_HEAD_DIM
GRID_W = 64
NA_KH_MAX = 8
NA_KW = 16
NA_QC = 16
NA_KC = NA_QC + NA_KW
IN_SPLITS = (D_SGU, 2 * D_SGU, 3 * D_SGU, 3 * D_SGU + D_NA, 3 * D_SGU + 2 * D_NA, 3 * D_SGU + 3 * D_NA)
D_IN = 3 * D_SGU + 4 * D_NA
DEEPNORM_ALPHA = (2 * DEPTH) ** 0.25
DEEPNORM_BETA = (8 * DEPTH) ** -0.25
ADA_SCALE = 0.1
LN_EPS = 1e-5

kernel_name = "hybrid_sgu_natten_deepnorm_adaln"


def _layernorm(x, g=None, b=None):
    xf = x.astype(jnp.float32)
    mu = jnp.mean(xf, axis=-1, keepdims=True)
    var = jnp.mean(jnp.square(xf - mu), axis=-1, keepdims=True)
    y = (xf - mu) * lax.rsqrt(var + LN_EPS)
    if g is not None:
        y = y * g.astype(jnp.float32) + b.astype(jnp.float32)
    return y.astype(x.dtype)


def _spatial_gating(u, v, z, norm_g, norm_b, w_s, b_s):
    B, T, _ = u.shape
    n_chunks = T // SGU_CHUNK
    vg = v.reshape(B, T, SGU_GROUPS, SGU_GROUP)
    vg = _layernorm(vg, norm_g.reshape(SGU_GROUPS, SGU_GROUP), norm_b.reshape(SGU_GROUPS, SGU_GROUP))
    vc = vg.reshape(B, n_chunks, SGU_CHUNK, SGU_GROUPS, SGU_GROUP)
    sv = jnp.einsum('gpq,bnqgc->bnpgc', w_s, vc) + b_s.T[None, None, :, :, None]
    return u * sv.reshape(B, T, D_SGU) * jax.nn.silu(z)


def _na_tables(rows):
    kh = min(NA_KH_MAX, rows)
    r = np.arange(rows)
    row_start = np.clip(r - kh // 2, 0, rows - kh)
    row_bias_idx = row_start[:, None] + np.arange(kh)[None, :] - r[:, None] + NA_KH_MAX - 1
    ncb = GRID_W // NA_QC
    qc0 = np.arange(ncb) * NA_QC
    band_start = np.clip(qc0 - NA_KW // 2, 0, GRID_W - NA_KC)
    band_cols = band_start[:, None] + np.arange(NA_KC)[None, :]
    q_cols = qc0[:, None] + np.arange(NA_QC)[None, :]
    win_start = np.clip(q_cols - NA_KW // 2, 0, GRID_W - NA_KW)
    col_valid = ((band_cols[:, None, :] >= win_start[:, :, None])
                 & (band_cols[:, None, :] < win_start[:, :, None] + NA_KW))
    col_bias_idx = np.clip(band_cols[:, None, :] - q_cols[:, :, None] + NA_KW - 1, 0, 2 * NA_KW - 2)
    return (kh, row_start.astype(np.int32), row_bias_idx.astype(np.int32),
            band_cols.astype(np.int32), col_valid, col_bias_idx.astype(np.int32))


def _neighbourhood_attention(q, k, v, rpb):
    B, T, H, dh = q.shape
    rows = T // GRID_W
    kh, row_start, row_bias_idx, band_cols, col_valid, col_bias_idx = _na_tables(rows)
    ncb = GRID_W // NA_QC
    to_grid = lambda a: a.reshape(B, rows, GRID_W, H, dh).transpose(0, 3, 1, 2, 4)
    kg, vg = to_grid(k), to_grid(v)
    q_rows = to_grid(q).reshape(B, H, rows, ncb, NA_QC, dh).transpose(2, 0, 1, 3, 4, 5)
    valid = jnp.asarray(col_valid)[None, None, :, :, None, :]
    col_bias_idx = jnp.asarray(col_bias_idx)[:, :, None, :]
    band_cols = jnp.asarray(band_cols)
    scale = dh ** -0.5

    def one_row(args):
        q_r, rs, rbi = args
        k_r = lax.dynamic_slice_in_dim(kg, rs, kh, axis=2)
        v_r = lax.dynamic_slice_in_dim(vg, rs, kh, axis=2)
        k_b = jnp.take(k_r, band_cols, axis=3)
        v_b = jnp.take(v_r, band_cols, axis=3)
        s = jnp.einsum('bhcqd,bhicxd->bhcqix', q_r, k_b).astype(jnp.float32) * scale
        bias = rpb[:, rbi[None, None, :, None], col_bias_idx]
        s = jnp.where(valid, s + bias[None].astype(jnp.float32), -1e30)
        p = jax.nn.softmax(s.reshape(B, H, ncb, NA_QC, kh * NA_KC), axis=-1)
        p = p.reshape(s.shape).astype(v.dtype)
        return jnp.einsum('bhcqix,bhicxd->bhcqd', p, v_b)

    out = lax.map(one_row, (q_rows, jnp.asarray(row_start), jnp.asarray(row_bias_idx)))
    out = out.reshape(rows, B, H, GRID_W, dh).transpose(1, 0, 3, 2, 4)
    return out.reshape(B, T, H * dh)


def setup_inputs(seed: int = 0) -> dict:
    key = jax.random.key(seed)
    ks = jax.random.split(key, 14)
    f32 = jnp.float32
    n = lambda k, s: jax.random.normal(k, s, f32)
    return {
        "x": n(ks[0], (BATCH, SEQ, D_MODEL)),
        "c": n(ks[1], (BATCH, D_MODEL)),
        "w_ada": n(ks[2], (DEPTH, D_MODEL, 3 * D_MODEL)) * (D_MODEL ** -0.5) * ADA_SCALE,
        "b_ada": n(ks[3], (DEPTH, 3 * D_MODEL)) * 0.01,
        "w_in": n(ks[4], (DEPTH, D_MODEL, D_IN)) * (D_MODEL ** -0.5),
        "sgu_norm_g": 1.0 + 0.02 * n(ks[5], (DEPTH, D_SGU)),
        "sgu_norm_b": 0.02 * n(ks[6], (DEPTH, D_SGU)),
        "w_spatial": n(ks[7], (DEPTH, SGU_GROUPS, SGU_CHUNK, SGU_CHUNK)) * (SGU_CHUNK ** -0.5),
        "b_spatial": 1.0 + 0.02 * n(ks[8], (DEPTH, SGU_GROUPS, SGU_CHUNK)),
        "rpb": 0.1 * n(ks[9], (DEPTH, NA_HEADS, 2 * NA_KH_MAX - 1, 2 * NA_KW - 1)),
        "w_out": n(ks[10], (DEPTH, D_MIX, D_MODEL)) * (D_MIX ** -0.5) * DEEPNORM_BETA,
        "ln_g": 1.0 + 0.02 * n(ks[11], (DEPTH, D_MODEL)),
        "ln_b": 0.02 * n(ks[12], (DEPTH, D_MODEL)),
    }


def reference(x, c, w_ada, b_ada, w_in, sgu_norm_g, sgu_norm_b, w_spatial, b_spatial, rpb, w_out, ln_g, ln_b):
    B, T, _ = x.shape
    for l in range(DEPTH):
        mod = jax.nn.silu(c) @ w_ada[l] + b_ada[l]
        shift, scale, gate = jnp.split(mod[:, None, :], 3, axis=-1)
        h = _layernorm(x) * (1.0 + scale) + shift
        proj = h @ w_in[l]
        u, v, z_a, q, k, v_b, z_b = jnp.split(proj, IN_SPLITS, axis=-1)
        y_a = _spatial_gating(jax.nn.gelu(u, approximate=False), jax.nn.gelu(v, approximate=False), z_a,
                              sgu_norm_g[l], sgu_norm_b[l], w_spatial[l], b_spatial[l])
        hs = (B, T, NA_HEADS, NA_HEAD_DIM)
        y_b = _neighbourhood_attention(q.reshape(hs), k.reshape(hs), v_b.reshape(hs), rpb[l]) * jax.nn.silu(z_b)
        y = jnp.concatenate([y_a, y_b], axis=-1) @ w_out[l]
        x = _layernorm(DEEPNORM_ALPHA * x + (1.0 + gate) * y, ln_g[l], ln_b[l])
    return x
```

```python
import contextlib
import numpy as np
import ml_dtypes
import concourse.bass as bass
import concourse.mybir as mybir
from concourse.bass_utils import run_bass_kernel_spmd

F32 = mybir.dt.float32
BF16 = mybir.dt.bfloat16
AF = mybir.ActivationFunctionType
ALU = mybir.AluOpType

D = 2048
KC = 16
DEPTH = 2
ALPHA = float((2 * DEPTH) ** 0.25)
EPS = 1e-5
NEG = -30000.0
OWN = 16
N_CORES = 8


class Sched:
    ENGS = ("pe", "act", "dve", "pool", "sp")

    def __init__(self, nc, stack):
        self.nc = nc
        self.stack = stack
        self.ops = {e: [] for e in self.ENGS}
        self.esem = {e: stack.enter_context(nc.semaphore("se_" + e)) for e in ("pe", "act", "dve", "pool")}
        self.ecnt = {e: 0 for e in self.esem}
        self.dsem = {}
        self.lastw = {}
        self.rd = {}
        self.waited = {}
        self.n_wait = 0

    def _collect(self, eng, reads, writes):
        need = {}

        def add(tok, raw):
            sem, val, owner = tok
            if owner == eng:
                if eng == "pe" or not raw:
                    return
            k = id(sem)
            if self.waited.get((eng, k), 0) >= val:
                return
            if k not in need or need[k][1] < val:
                need[k] = (sem, val)

        for key in reads:
            for t in self.lastw.get(key, {}).values():
                add(t, True)
        for key in writes:
            for t in self.lastw.get(key, {}).values():
                add(t, False)
            for t in self.rd.get(key, {}).values():
                add(t, False)
        out = list(need.values())
        for sem, val in out:
            self.waited[(eng, id(sem))] = val
        self.n_wait += len(out)
        return out

    def _record(self, tok, reads, writes):
        k = id(tok[0])
        for key in writes:
            d = self.lastw.setdefault(key, {})
            if k not in d or d[k][1] < tok[1]:
                d[k] = tok
        for key in reads:
            d = self.rd.setdefault(key, {})
            if k not in d or d[k][1] < tok[1]:
                d[k] = tok

    def op(self, eng, fns, reads=(), writes=()):
        if not isinstance(fns, (list, tuple)):
            fns = [fns]
        waits = self._collect(eng, reads, writes)
        self.ecnt[eng] += 1
        sem = self.esem[eng]
        val = self.ecnt[eng]
        tok = (sem, val, eng)
        self._record(tok, reads, writes)

        def thunk(e, waits=waits, fns=fns, sem=sem):
            for s, v in waits:
                e.wait_ge(s, v)
            for f in fns[:-1]:
                f(e)
            fns[-1](e).then_inc(sem, 1)

        self.ops[eng].append(thunk)
        return tok

    def dma(self, q, out, in_, owner, reads=(), writes=()):
        if owner not in self.dsem:
            self.dsem[owner] = [self.stack.enter_context(self.nc.semaphore("sd_" + owner)), 0]
        ent = self.dsem[owner]
        waits = self._collect(q, reads, writes)
        ent[1] += 16
        sem, val = ent[0], ent[1]
        tok = (sem, val, None)
        self._record(tok, reads, writes)

        def thunk(e, waits=waits, sem=sem, out=out, in_=in_):
            for s, v in waits:
                e.wait_ge(s, v)
            e.dma_start(out=out, in_=in_).then_inc(sem, 16)

        self.ops[q].append(thunk)
        return tok

    def wait_all(self, eng, toks):
        need = {}
        for sem, val, _ in toks:
            k = id(sem)
            if k not in need or need[k][1] < val:
                need[k] = (sem, val)
        lst = list(need.values())

        def thunk(e, lst=lst):
            for s, v in lst:
                e.wait_ge(s, v)

        self.ops[eng].append(thunk)

    def finish(self):
        with self.nc.Block() as block:
            @block.sync
            def _(e):
                for t in self.ops["sp"]:
                    t(e)

            @block.tensor
            def _(e):
                for t in self.ops["pe"]:
                    t(e)

            @block.scalar
            def _(e):
                for t in self.ops["act"]:
                    t(e)

            @block.vector
            def _(e):
                for t in self.ops["dve"]:
                    t(e)

            @block.gpsimd
            def _(e):
                for t in self.ops["pool"]:
                    t(e)


def chunks(total, size):
    out = []
    c = 0
    while c < total:
        n = min(size, total - c)
        out.append((c, n))
        c += n
    return out


def build_program(layer_ids, NP0, PQ=4, wq="pool", OWN=OWN):
    nc = bass.Bass("TRN2", target_bir_lowering=False)
    nL = len(layer_ids)
    NPs = [NP0 - 4 * k for k in range(nL)]
    NOUT = NP0 - 4 * nL
    assert NOUT == OWN
    PK = PQ + 4
    TQ, TK = PQ * 128, PK * 128

    def din(name, shape, dt=F32):
        return nc.dram_tensor(name, list(shape), dt, kind="ExternalInput").ap()

    xw = din("xw", [NP0 * 128, D])
    c_r = din("c_r", [128, KC])
    identb_d = din("identb", [128, 128], BF16)
    identf_d = din("identf", [128, 128])
    ones_d = din("ones_row", [1, 1024])
    kaug_d = din("kaug", [32, NP0 * 128], BF16)
    L = []
    for k in range(nL):
        L.append(dict(
            wss=din(f"wss_{k}", [32, 128, KC, 128]),
            wg=din(f"wg_{k}", [4, 128, KC, 512]),
            bss=din(f"bss_{k}", [128, 32]),
            bg=din(f"bg_{k}", [128, D]),
            win=din(f"win_{k}", [56, 128, KC, 128]),
            wout=din(f"wout_{k}", [4, 128, KC, 512]),
            wsT=din(f"wsT_{k}", [128, 1024]),
            gam=din(f"gam_{k}", [128, 8]),
            bet=din(f"bet_{k}", [1, 1024]),
            bs=din(f"bs_{k}", [1, 1024]),
            bias=din(f"bias_{k}", [16, 128, 7 * 128]),
            lng=din(f"lng_{k}", [128, D]),
            lnb=din(f"lnb_{k}", [128, D]),
            qaug=din(f"qaug_{k}", [32, NPs[k] * 128], BF16),
        ))
    out_d = nc.dram_tensor("out", [OWN * 128, D], F32, kind="ExternalOutput").ap()
    mids = [nc.dram_tensor(f"xmid_{k}", [NPs[k + 1] * 128, D], F32, kind="Internal").ap() for k in range(nL - 1)]

    with contextlib.ExitStack() as st:
        S = Sched(nc, st)

        def sb(name, shape, dt=F32):
            return st.enter_context(nc.sbuf_tensor(name, list(shape), dt))

        def ps(name, shape, dt=F32):
            return st.enter_context(nc.psum_tensor(name, list(shape), dt))

        hT = sb("hT", [128, KC, TK], BF16)
        yT = sb("yT", [128, KC, TQ], BF16)
        XT = [sb(f"xt{i}", [128, D]) for i in range(2)]
        xn = sb("xn", [128, D], BF16)
        stt = [sb(f"stt{i}", [128, 4, 6]) for i in range(2)]
        mv = [sb(f"mv{i}", [128, 4, 2]) for i in range(2)]
        ve = [sb(f"ve{i}", [128, 4]) for i in range(2)]
        rstd = [sb(f"rstd{i}", [128, 4]) for i in range(2)]
        nmr = [sb(f"nmr{i}", [128, 4]) for i in range(2)]
        neghalf = sb("neghalf", [128, 4])
        sc1p = sb("sc1p", [128, KC])
        shift = sb("shift", [128, KC])
        NW = 3
        WB = [sb(f"wb{i}", [128, KC, 512], BF16) for i in range(NW)]
        BT = [sb(f"bt{i}", [128, 7 * 128]) for i in range(2)]
        QA = sb("QA", [128, TQ], BF16)
        QB = sb("QB", [128, TQ], BF16)
        KA = sb("KA", [128, TK], BF16)
        KB = sb("KB", [128, TK], BF16)
        Vt = sb("Vt", [128, PK, 130], BF16)
        SZ = sb("SZ", [128, TQ])
        GU = sb("GU", [128, TQ])
        Ssb = [sb(f"Ssb{i}", [128, 768]) for i in range(2)]
        PT = [sb(f"PT{i}", [128, 768], BF16) for i in range(2)]
        yb = sb("yb", [128, PQ, 128])
        rden = sb("rden", [128, PQ])
        gv = [sb(f"gv{i}", [128, 4, 128]) for i in range(2)]
        vn = [sb(f"vn{i}", [128, 4, 128], BF16) for i in range(2)]
        svt = [sb(f"svt{i}", [128, 512]) for i in range(2)]
        g1p = sb("g1p", [128, D])
        lng = sb("lng", [128, D])
        lnb = sb("lnb", [128, D])
        gam = sb("gam", [128, 8])
        bias2 = sb("bias2", [128, 1024])
        WsTb = sb("WsTb", [128, 1024], BF16)
        L2 = sb("L2", [2, 1024])
        R2 = sb("R2", [2, 1024])
        identb = sb("identb_s", [128, 128], BF16)
        identf = sb("identf_s", [128, 128])
        ones_col = sb("ones_col", [128, 1])
        c_sb = sb("c_sb", [128, KC])
        scb = sb("scb", [128, KC], BF16)
        bss = sb("bss", [128, 32])
        TT = [sb(f"tt{i}", [128, 512]) for i in range(3)]
        XS = [sb(f"xs{i}", [128, 512]) for i in range(2)]
        st4 = sb("st4", [128, PQ, 4, 6])

        bgs = XT[0]
        WsTf = XT[1]
        sc_rep = xn[:].rearrange("p (k m) -> p k m", m=128)
        psP = [ps(f"psP{i}", [128, 512]) for i in range(2)]
        psS = [ps(f"psS{i}", [128, 1024]) for i in range(2)]
        psACC = ps("psACC", [128, 512])
        psT = ps("psT", [128, 1024], BF16)

        ctr = {"w": 0, "p": 0, "s": 0, "t": 0, "x": 0, "g": 0, "sv": 0, "l": 0}

        def rot(name, n):
            v = ctr[name] % n
            ctr[name] += 1
            return v

        S.dma("sp", identb[:], identb_d, "identb", writes=["identb"])
        S.dma("sp", identf[:], identf_d, "identf", writes=["identf"])
        S.dma("sp", c_sb[:], c_r, "c_sb", writes=["c_sb"])
        S.op("dve", lambda e: e.memset(ones_col[:], 1.0), writes=["ones_col"])
        S.op("dve", lambda e: e.memset(neghalf[:], -0.5), writes=["neghalf"])
        S.op("dve", lambda e: e.memset(Vt[:], 1.0), writes=["Vt"])
        for tl, nm in ((QA, "QA"), (QB, "QB"), (KA, "KA"), (KB, "KB")):
            S.op("pool", lambda e, tl=tl: e.memset(tl[:], 0.0), writes=[nm, nm + "aug"])
        S.op("act", lambda e: e.activation(out=c_sb[:], in_=c_sb[:], func=AF.Silu), reads=["c_sb"], writes=["c_sb"])
        S.op("dve", lambda e: e.tensor_copy(out=scb[:], in_=c_sb[:]), reads=["c_sb"], writes=["scb"])
        S.op("dve", lambda e: e.tensor_copy(out=sc_rep, in_=c_sb[:, :, None].broadcast_to([128, KC, 128])),
             reads=["c_sb"], writes=["xn"])

        wjobs = []
        for k_ in range(nL):
            for j0 in range(0, 32, 4):
                wjobs.append([(s4 * 128, 128, L[k_]["wss"][j0 + s4]) for s4 in range(4)])
            for cg_ in range(4):
                wjobs.append([(0, 512, L[k_]["wg"][cg_])])
            for _u in chunks(NPs[k_] - 4, PQ):
                for hp_ in range(8):
                    wjobs.append([(s4 * 128, 128, L[k_]["win"][sl]) for s4, sl in enumerate((24 + hp_, 32 + hp_, 40 + hp_, 48 + hp_))])
                for g_ in range(8):
                    wjobs.append([(s4 * 128, 128, L[k_]["win"][sl]) for s4, sl in enumerate((g_, 8 + g_, 16 + g_))])
                for cg_ in range(4):
                    wjobs.append([(0, 512, L[k_]["wout"][cg_])])
        wstate = {"issued": 0, "next": 0}

        def next_w():
            idx = wstate["next"]
            wstate["next"] += 1
            while wstate["issued"] < min(len(wjobs), idx + NW):
                j = wstate["issued"]
                bi = j % NW
                for (c0, w_, src_ap) in wjobs[j]:
                    S.dma(wq, WB[bi][:, :, c0:c0 + w_], src_ap, f"wb{bi}", writes=[f"wb{bi}"])
                wstate["issued"] += 1
            return WB[idx % NW], f"wb{idx % NW}"

        def rstd_from_var(par, n, var_ap, mean_ap):
            S.op("dve", lambda e: e.tensor_scalar(out=ve[par][:, 0:n], in0=var_ap, scalar1=EPS, scalar2=None,
                                                   op0=ALU.add), reads=[f"mv{par}"], writes=[f"ve{par}"])
            S.op("pool", lambda e: e.tensor_tensor(out=rstd[par][:, 0:n], in0=ve[par][:, 0:n],
                                                    in1=neghalf[:, 0:n], op=ALU.pow),
                 reads=[f"ve{par}", "neghalf"], writes=[f"rstd{par}"])
            S.op("dve", lambda e: e.scalar_tensor_tensor(out=nmr[par][:, 0:n], in0=mean_ap, scalar=-1.0,
                                                          in1=rstd[par][:, 0:n], op0=ALU.mult, op1=ALU.mult),
                 reads=[f"mv{par}", f"rstd{par}"], writes=[f"nmr{par}"])

        def proj_fm(wb, wbname, wcol, tok0, ntok, hkeys, evac):
            for (c0, n) in chunks(ntok, 512):
                b = rot("p", 2)
                fns = []
                for kc in range(KC):
                    fns.append(lambda e, kc=kc, b=b, c0=c0, n=n: e.matmul(
                        psP[b][:, 0:n], lhsT=wb[:, kc, wcol:wcol + 128], rhs=hT[:, kc, tok0 + c0:tok0 + c0 + n],
                        start=(kc == 0), stop=(kc == KC - 1)))
                S.op("pe", fns, reads=[wbname] + hkeys, writes=[f"psP{b}"])
                evac(psP[b], f"psP{b}", c0, n)

        final_toks = []
        for k in range(nL):
            Lk = L[k]
            NP = NPs[k]
            NQ = NP - 4
            src = xw if k == 0 else mids[k - 1]
            dst = out_d if k == nL - 1 else mids[k]
            skey = (lambda t, k=k: ("src", k, t))
            dkey = (lambda t, k=k: ("src", k + 1, t))
            off = (NP - OWN) // 2
            extra = {off + 3: off, off + OWN - 4: off + OWN - 1}

            cs = f"c{k + 1}"
            S.dma("sp", bss[:], Lk["bss"], "bss", writes=["bss"])
            S.dma("sp", bgs[:], Lk["bg"], "xt0", writes=["xt0"])
            S.dma("sp", lng[:], Lk["lng"], "lng", writes=["lng"])
            S.dma("sp", lnb[:], Lk["lnb"], "lnb", writes=["lnb"])
            S.dma("sp", gam[:], Lk["gam"], "gam", writes=["gam"])
            S.dma("sp", WsTf[:, 0:1024], Lk["wsT"], "xt1", writes=["xt1"])
            S.dma("sp", L2[0:1, :], Lk["bet"], "L2", writes=["L2"])
            S.dma("sp", L2[1:2, :], ones_d, "L2", writes=["L2"])
            S.dma("sp", R2[1:2, :], Lk["bs"], "R2", writes=["R2b"])
            S.op("dve", lambda e: e.tensor_copy(out=WsTb[:], in_=WsTf[:, 0:1024]), reads=["xt1"], writes=["WsTb"])
            for h2 in range(2):
                b = rot("p", 2)
                S.op("pe", lambda e, b=b, h2=h2: e.matmul(psP[b][0:1, :], lhsT=ones_col[:, 0:1],
                                                          rhs=WsTf[:, h2 * 512:(h2 + 1) * 512], start=True, stop=True),
                     reads=["ones_col", "xt1"], writes=[f"psP{b}"])
                S.op("dve", lambda e, b=b, h2=h2: e.tensor_copy(out=R2[0:1, h2 * 512:(h2 + 1) * 512], in_=psP[b][0:1, :]),
                     reads=[f"psP{b}"], writes=["R2a"])
            for h2 in range(2):
                b = rot("p", 2)
                fns = []
                for gg in range(4):
                    g = h2 * 4 + gg
                    fns.append(lambda e, b=b, g=g, gg=gg: e.matmul(
                        psP[b][:, gg * 128:(gg + 1) * 128], lhsT=L2[0:2, g * 128:(g + 1) * 128],
                        rhs=R2[0:2, g * 128:(g + 1) * 128], start=True, stop=True))
                S.op("pe", fns, reads=["L2", "R2a", "R2b"], writes=[f"psP{b}"])
                S.op("dve", lambda e, b=b, h2=h2: e.tensor_copy(out=bias2[:, h2 * 512:(h2 + 1) * 512], in_=psP[b][:]),
                     reads=[f"psP{b}"], writes=["bias2"])
            b = rot("p", 2)
            for jc in range(32):
                if jc % 4 == 0:
                    wb, wbn = next_w()
                s4 = jc % 4
                fns = []
                for kc in range(KC):
                    fns.append(lambda e, kc=kc, s4=s4, wb=wb, jc=jc, b=b: e.matmul(
                        psP[b][:, jc:jc + 1], lhsT=wb[:, kc, s4 * 128:(s4 + 1) * 128], rhs=scb[:, kc:kc + 1],
                        start=(kc == 0), stop=(kc == KC - 1)))
                S.op("pe", fns, reads=[wbn, "scb"], writes=[f"psP{b}"])
            S.op("dve", lambda e, b=b: e.tensor_tensor(out=shift[:], in0=psP[b][:, 0:16], in1=bss[:, 0:16], op=ALU.add),
                 reads=[f"psP{b}", "bss"], writes=["shift"])
            S.op("dve", lambda e, b=b: e.scalar_tensor_tensor(out=sc1p[:], in0=psP[b][:, 16:32], scalar=1.0,
                                                              in1=bss[:, 16:32], op0=ALU.add, op1=ALU.add),
                 reads=[f"psP{b}", "bss"], writes=["sc1p"])
            for cg in range(4):
                wb, wbn = next_w()
                b = rot("p", 2)
                fns = []
                for kc in range(KC):
                    fns.append(lambda e, kc=kc, wb=wb, b=b: e.matmul(
                        psP[b][:], lhsT=sc_rep[:, kc, :], rhs=wb[:, kc, :], start=(kc == 0), stop=(kc == KC - 1)))
                S.op("pe", fns, reads=[wbn, "xn"], writes=[f"psP{b}"])
                S.op("dve", lambda e, b=b, cg=cg: e.scalar_tensor_tensor(
                    out=g1p[:, cg * 512:(cg + 1) * 512], in0=psP[b][:], scalar=1.0,
                    in1=bgs[:, cg * 512:(cg + 1) * 512], op0=ALU.add, op1=ALU.add),
                    reads=[f"psP{b}", "xt0"], writes=["g1p"])

            for (u0, pq) in chunks(NQ, PQ):
                q0 = 2 + u0
                k0 = q0 - 2
                pk = pq + 4
                tq = pq * 128
                hkeys = [("hT", t) for t in range(pk)]

                S.dma("sp", QA[64:96, 0:tq], Lk["qaug"][:, q0 * 128:q0 * 128 + tq], "QA", writes=["QAaug"])
                S.dma("sp", QB[32:64, 0:tq], Lk["qaug"][:, q0 * 128:q0 * 128 + tq], "QB", writes=["QBaug"])
                koff = (NP0 - NP) // 2 + k0
                S.dma("sp", KA[64:96, 0:pk * 128], kaug_d[:, koff * 128:(koff + pk) * 128], "KA", writes=["KAaug"])
                S.dma("sp", KB[32:64, 0:pk * 128], kaug_d[:, koff * 128:(koff + pk) * 128], "KB", writes=["KBaug"])

                for t in range(pk):
                    xi = rot("x", 2)
                    xt, xtn = XT[xi], f"xt{xi}"
                    par = rot("s", 2)
                    S.dma("sp", xt[:], src[(k0 + t) * 128:(k0 + t + 1) * 128, :], xtn, reads=[skey(k0 + t)], writes=[xtn])
                    for c4 in range(4):
                        S.op("dve", lambda e, c4=c4, xt=xt, par=par: e.bn_stats(out=stt[par][:, c4, :], in_=xt[:, c4 * 512:(c4 + 1) * 512]),
                             reads=[xtn], writes=[f"stt{par}"])
                    S.op("dve", lambda e, par=par: e.bn_aggr(out=mv[par][:, 0, :], in_=stt[par][:].rearrange("p a b -> p (a b)")),
                         reads=[f"stt{par}"], writes=[f"mv{par}"])
                    rstd_from_var(par, 1, mv[par][:, 0, 1:2], mv[par][:, 0, 0:1])
                    S.op("act", lambda e, xt=xt, par=par: e.activation(out=xn[:], in_=xt[:], func=AF.Identity,
                                                                        bias=nmr[par][:, 0:1], scale=rstd[par][:, 0:1]),
                         reads=[xtn, f"rstd{par}", f"nmr{par}"], writes=["xn"])
                    for half in range(2):
                        fns = []
                        for j in range(8):
                            kc = half * 8 + j
                            fns.append(lambda e, j=j, kc=kc: e.transpose(psT[:, j * 128:(j + 1) * 128],
                                                                         xn[:, kc * 128:(kc + 1) * 128], identb[:]))
                        S.op("pe", fns, reads=["xn", "identb"], writes=["psT"])
                        for j in range(8):
                            kc = half * 8 + j
                            if j % 2 == 0:
                                S.op("dve", lambda e, j=j, kc=kc, t=t: e.tensor_scalar(
                                    out=hT[:, kc, t * 128:(t + 1) * 128], in0=psT[:, j * 128:(j + 1) * 128],
                                    scalar1=sc1p[:, kc:kc + 1], scalar2=shift[:, kc:kc + 1], op0=ALU.mult, op1=ALU.add),
                                    reads=["psT", "sc1p", "shift"], writes=[("hT", t)])
                            else:
                                S.op("act", lambda e, j=j, kc=kc, t=t: e.activation(
                                    out=hT[:, kc, t * 128:(t + 1) * 128], in_=psT[:, j * 128:(j + 1) * 128],
                                    func=AF.Identity, bias=shift[:, kc:kc + 1], scale=sc1p[:, kc:kc + 1]),
                                    reads=["psT", "sc1p", "shift"], writes=[("hT", t)])

                for hp in range(8):
                    wb, wbn = next_w()
                    for hd in range(2):
                        S.dma("sp", BT[hd][:], Lk["bias"][2 * hp + hd], f"bt{hd}", writes=[f"bt{hd}"])

                    def ev_q(pt, pn, c0, n):
                        S.op("act", lambda e: e.activation(out=QA[0:64, c0:c0 + n], in_=pt[0:64, 0:n], func=AF.Copy, scale=0.125),
                             reads=[pn], writes=["QA"])
                        S.op("dve", lambda e: e.tensor_scalar(out=QB[64:128, c0:c0 + n], in0=pt[64:128, 0:n], scalar1=0.125,
                                                               scalar2=None, op0=ALU.mult), reads=[pn], writes=["QB"])

                    def ev_k(pt, pn, c0, n):
                        S.op("act", lambda e: e.activation(out=KA[0:64, c0:c0 + n], in_=pt[0:64, 0:n], func=AF.Copy),
                             reads=[pn], writes=["KA"])
                        S.op("dve", lambda e: e.tensor_copy(out=KB[64:128, c0:c0 + n], in_=pt[64:128, 0:n]),
                             reads=[pn], writes=["KB"])

                    def ev_z(pt, pn, c0, n):
                        S.op("act", lambda e: e.activation(out=SZ[:, c0:c0 + n], in_=pt[:, 0:n], func=AF.Silu),
                             reads=[pn], writes=["SZ"])

                    proj_fm(wb, wbn, 0, 256, tq, hkeys, ev_q)
                    proj_fm(wb, wbn, 128, 0, pk * 128, hkeys, ev_k)
                    for t in range(pk):
                        b = rot("p", 2)
                        fns = []
                        for kc in range(KC):
                            fns.append(lambda e, kc=kc, b=b, t=t, wb=wb: e.matmul(
                                psP[b][:, 0:128], lhsT=hT[:, kc, t * 128:(t + 1) * 128], rhs=wb[:, kc, 256:384],
                                start=(kc == 0), stop=(kc == KC - 1)))
                        S.op("pe", fns, reads=[wbn, ("hT", t)], writes=[f"psP{b}"])
                        S.op("dve", lambda e, b=b, t=t: e.tensor_copy(
                            out=Vt[:, t, :].rearrange("p (h e) -> p h e", e=65)[:, :, 0:64],
                            in_=psP[b][:, 0:128].rearrange("p (h d) -> p h d", d=64)),
                            reads=[f"psP{b}"], writes=["Vt"])
                    proj_fm(wb, wbn, 384, 256, tq, hkeys, ev_z)

                    for hd in range(2):
                        Qt, Kt = (QA, KA) if hd == 0 else (QB, KB)
                        qn, kn = ("QA", "KA") if hd == 0 else ("QB", "KB")
                        prs = slice(0, 128)
                        S.op("dve", lambda e: e.memset(psACC[:], 0.0), writes=["psACC"])
                        for jj in range(pk):
                            iq = jj - 2
                            i_lo, i_hi = max(0, iq - 2), min(pq - 1, iq + 2)
                            xq = extra.get(k0 + jj)
                            if xq is not None and q0 <= xq < q0 + pq:
                                xi_ = xq - q0
                                assert xi_ in (i_lo - 1, i_hi + 1), (xi_, i_lo, i_hi)
                                i_lo, i_hi = min(i_lo, xi_), max(i_hi, xi_)
                            if i_hi < i_lo:
                                continue
                            nb = i_hi - i_lo + 1
                            ncol = nb * 128
                            b0 = 3 - (iq - i_lo)
                            assert 0 <= b0 and b0 + nb <= 7
                            sb_ = rot("s", 2)
                            fns = []
                            for (c0, n) in chunks(ncol, 512):
                                fns.append(lambda e, c0=c0, n=n, sb_=sb_, jj=jj, i_lo=i_lo, Kt=Kt, Qt=Qt, prs=prs: e.matmul(
                                    psS[sb_][:, c0:c0 + n], lhsT=Kt[prs, jj * 128:(jj + 1) * 128],
                                    rhs=Qt[prs, i_lo * 128 + c0:i_lo * 128 + c0 + n], start=True, stop=True))
                            S.op("pe", fns, reads=[qn, kn, qn + "aug", kn + "aug"], writes=[f"psS{sb_}"])
                            S.op("dve", lambda e, sb_=sb_, ncol=ncol, b0=b0, hd=hd: e.tensor_tensor(
                                out=Ssb[sb_][:, 0:ncol], in0=psS[sb_][:, 0:ncol], in1=BT[hd][:, b0 * 128:b0 * 128 + ncol], op=ALU.add),
                                reads=[f"psS{sb_}", f"bt{hd}"], writes=[f"Ssb{sb_}"])
                            S.op("act", lambda e, sb_=sb_, ncol=ncol: e.activation(out=PT[sb_][:, 0:ncol], in_=Ssb[sb_][:, 0:ncol], func=AF.Exp),
                                 reads=[f"Ssb{sb_}"], writes=[f"PT{sb_}"])
                            fns = []
                            for i in range(i_lo, i_hi + 1):
                                fns.append(lambda e, i=i, i_lo=i_lo, sb_=sb_, jj=jj, hd=hd: e.matmul(
                                    psACC[:, i * 65:(i + 1) * 65], lhsT=PT[sb_][:, (i - i_lo) * 128:(i - i_lo + 1) * 128],
                                    rhs=Vt[:, jj, hd * 65:(hd + 1) * 65], start=False, stop=False, skip_group_check=True))
                            S.op("pe", fns, reads=[f"PT{sb_}", "Vt"], writes=["psACC"])
                        accv = psACC[:, 0:pq * 65].rearrange("p (i e) -> p i e", e=65)
                        S.op("dve", lambda e, accv=accv, pq=pq: e.reciprocal(out=rden[:, 0:pq], in_=accv[:, :, 64]),
                             reads=["psACC"], writes=["rden"])
                        S.op("dve", lambda e, accv=accv, hd=hd, pq=pq: e.tensor_tensor(
                            out=yb[:, 0:pq, hd * 64:(hd + 1) * 64], in0=accv[:, :, 0:64],
                            in1=rden[:, 0:pq, None].broadcast_to([128, pq, 64]), op=ALU.mult),
                            reads=["psACC", "rden"], writes=["yb"])
                    for (i0, n) in chunks(pq, 4):
                        b = rot("p", 2)
                        fns = []
                        for j in range(n):
                            fns.append(lambda e, j=j, i0=i0, b=b: e.transpose(psP[b][:, j * 128:(j + 1) * 128], yb[:, i0 + j, :], identf[:]))
                        S.op("pe", fns, reads=["yb", "identf"], writes=[f"psP{b}"])
                        S.op("dve", lambda e, b=b, i0=i0, n=n, hp=hp: e.tensor_tensor(
                            out=yT[:, 8 + hp, i0 * 128:(i0 + n) * 128], in0=psP[b][:, 0:n * 128],
                            in1=SZ[:, i0 * 128:(i0 + n) * 128], op=ALU.mult),
                            reads=[f"psP{b}", "SZ"], writes=["yT"])

                for g in range(8):
                    wb, wbn = next_w()

                    def ev_u(pt, pn, c0, n):
                        S.op("act", lambda e: e.activation(out=GU[:, c0:c0 + n], in_=pt[:, 0:n], func=AF.Gelu),
                             reads=[pn], writes=["GU"])

                    def ev_za(pt, pn, c0, n):
                        S.op("act", lambda e: e.activation(out=SZ[:, c0:c0 + n], in_=pt[:, 0:n], func=AF.Silu),
                             reads=[pn], writes=["SZ"])

                    proj_fm(wb, wbn, 0, 256, tq, hkeys, ev_u)
                    proj_fm(wb, wbn, 256, 256, tq, hkeys, ev_za)
                    S.op("pool", lambda e, tq=tq: e.tensor_tensor(out=GU[:, 0:tq], in0=GU[:, 0:tq], in1=SZ[:, 0:tq], op=ALU.mult),
                         reads=["GU", "SZ"], writes=["GU"])
                    for (i0, n) in chunks(pq, 4):
                        b = rot("p", 2)
                        gi = rot("g", 2)
                        par = rot("s", 2)
                        fns = []
                        for j in range(n):
                            for kc in range(KC):
                                fns.append(lambda e, j=j, kc=kc, b=b, i0=i0, wb=wb: e.matmul(
                                    psP[b][:, j * 128:(j + 1) * 128], lhsT=hT[:, kc, (2 + i0 + j) * 128:(3 + i0 + j) * 128],
                                    rhs=wb[:, kc, 128:256], start=(kc == 0), stop=(kc == KC - 1)))
                        S.op("pe", fns, reads=[wbn] + hkeys, writes=[f"psP{b}"])
                        S.op("act", lambda e, b=b, gi=gi, n=n: e.activation(
                            out=gv[gi][:, 0:n, :].rearrange("p a b -> p (a b)"), in_=psP[b][:, 0:n * 128], func=AF.Gelu),
                            reads=[f"psP{b}"], writes=[f"gv{gi}"])
                        for j in range(n):
                            S.op("dve", lambda e, j=j, gi=gi, par=par: e.bn_stats(out=stt[par][:, j, :], in_=gv[gi][:, j, :]),
                                 reads=[f"gv{gi}"], writes=[f"stt{par}"])
                            S.op("dve", lambda e, j=j, par=par: e.bn_aggr(out=mv[par][:, j, :], in_=stt[par][:, j, :]),
                                 reads=[f"stt{par}"], writes=[f"mv{par}"])
                        rstd_from_var(par, n, mv[par][:, 0:n, 1], mv[par][:, 0:n, 0])
                        for j in range(n):
                            S.op("dve", lambda e, j=j, gi=gi, par=par: e.tensor_scalar(
                                out=vn[gi][:, j, :], in0=gv[gi][:, j, :], scalar1=rstd[par][:, j:j + 1],
                                scalar2=nmr[par][:, j:j + 1], op0=ALU.mult, op1=ALU.add),
                                reads=[f"gv{gi}", f"rstd{par}", f"nmr{par}"], writes=[f"vn{gi}"])
                        b2 = rot("p", 2)
                        fns = []
                        for j in range(n):
                            fns.append(lambda e, j=j, b2=b2, gi=gi, g=g: e.matmul(
                                psP[b2][:, j * 128:(j + 1) * 128], lhsT=vn[gi][:, j, :], rhs=WsTb[:, g * 128:(g + 1) * 128],
                                start=True, stop=True))
                        S.op("pe", fns, reads=[f"vn{gi}", "WsTb"], writes=[f"psP{b2}"])
                        si = rot("sv", 2)
                        S.op("dve", lambda e, b2=b2, si=si, n=n, g=g: e.scalar_tensor_tensor(
                            out=svt[si][:, 0:n * 128].rearrange("p (a b) -> p a b", b=128),
                            in0=psP[b2][:, 0:n * 128].rearrange("p (a b) -> p a b", b=128),
                            scalar=gam[:, g:g + 1],
                            in1=bias2[:, None, g * 128:(g + 1) * 128].broadcast_to([128, n, 128]),
                            op0=ALU.mult, op1=ALU.add),
                            reads=[f"psP{b2}", "gam", "bias2"], writes=[f"svt{si}"])
                        S.op("pool", lambda e, si=si, i0=i0, n=n, g=g: e.tensor_tensor(
                            out=yT[:, g, i0 * 128:(i0 + n) * 128], in0=svt[si][:, 0:n * 128],
                            in1=GU[:, i0 * 128:(i0 + n) * 128], op=ALU.mult),
                            reads=[f"svt{si}", "GU"], writes=["yT"])

                for cg in range(4):
                    wb, wbn = next_w()
                    for i in range(pq):
                        b = rot("p", 2)
                        fns = []
                        for kc in range(KC):
                            fns.append(lambda e, kc=kc, b=b, i=i, wb=wb: e.matmul(
                                psP[b][:], lhsT=yT[:, kc, i * 128:(i + 1) * 128], rhs=wb[:, kc, :],
                                start=(kc == 0), stop=(kc == KC - 1)))
                        S.op("pe", fns, reads=[wbn, "yT"], writes=[f"psP{b}"])
                        xi = rot("l", 2)
                        S.dma("sp", XS[xi][:], src[(q0 + i) * 128:(q0 + i + 1) * 128, cg * 512:(cg + 1) * 512], f"xs{xi}",
                              reads=[skey(q0 + i)], writes=[f"xs{xi}"])
                        ti = rot("t", 3)
                        S.op("dve", lambda e, b=b, ti=ti, cg=cg: e.tensor_tensor(
                            out=TT[ti][:], in0=psP[b][:], in1=g1p[:, cg * 512:(cg + 1) * 512], op=ALU.mult),
                            reads=[f"psP{b}", "g1p"], writes=[f"tt{ti}"])
                        S.op("dve", lambda e, ti=ti, xi=xi: e.scalar_tensor_tensor(
                            out=TT[ti][:], in0=XS[xi][:], scalar=ALPHA, in1=TT[ti][:], op0=ALU.mult, op1=ALU.add),
                            reads=[f"xs{xi}", f"tt{ti}"], writes=[f"tt{ti}"])
                        S.op("dve", lambda e, ti=ti, i=i, cg=cg: e.bn_stats(out=st4[:, i, cg, :], in_=TT[ti][:]),
                             reads=[f"tt{ti}"], writes=[("st4", i)])
                        S.dma("sp", dst[(u0 + i) * 128:(u0 + i + 1) * 128, cg * 512:(cg + 1) * 512], TT[ti][:], f"tt{ti}",
                              reads=[f"tt{ti}"], writes=[("pre", k, u0 + i, cg)])

                for i in range(pq):
                    xi = rot("x", 2)
                    xt, xtn = XT[xi], f"xt{xi}"
                    par = rot("s", 2)
                    S.dma("sp", xt[:], dst[(u0 + i) * 128:(u0 + i + 1) * 128, :], xtn,
                          reads=[("pre", k, u0 + i, cg) for cg in range(4)], writes=[xtn])
                    S.op("dve", lambda e, i=i, par=par: e.bn_aggr(out=mv[par][:, 0, :], in_=st4[:, i, :, :].rearrange("p a b -> p (a b)")),
                         reads=[("st4", i)], writes=[f"mv{par}"])
                    rstd_from_var(par, 1, mv[par][:, 0, 1:2], mv[par][:, 0, 0:1])
                    S.op("act", lambda e, xt=xt, par=par: e.activation(out=xt[:], in_=xt[:], func=AF.Identity,
                                                                        bias=nmr[par][:, 0:1], scale=rstd[par][:, 0:1]),
                         reads=[xtn, f"rstd{par}", f"nmr{par}"], writes=[xtn])
                    S.op("dve", lambda e, xt=xt: e.tensor_tensor(out=xt[:], in0=xt[:], in1=lng[:], op=ALU.mult),
                         reads=[xtn, "lng"], writes=[xtn])
                    S.op("pool", lambda e, xt=xt: e.tensor_tensor(out=xt[:], in0=xt[:], in1=lnb[:], op=ALU.add),
                         reads=[xtn, "lnb"], writes=[xtn])
                    tok = S.dma("sp", dst[(u0 + i) * 128:(u0 + i + 1) * 128, :], xt[:], xtn + "st",
                                reads=[xtn] + [("pre", k, u0 + i, cg) for cg in range(4)], writes=[dkey(u0 + i)])
                    if k == nL - 1:
                        final_toks.append(tok)

        assert wstate["next"] == len(wjobs), (wstate, len(wjobs))
        S.wait_all("sp", final_toks)
        S.finish()
    return nc


def _slabs(w, width):
    K, C = w.shape
    a = w.reshape(KC, 128, C // width, width)
    return np.ascontiguousarray(a.transpose(2, 1, 0, 3))


def _bias_tables(rpb_l):
    cp = np.arange(64)
    c = np.arange(64)
    ws = np.clip(c - 8, 0, 48)
    colvalid = (cp[:, None] >= ws[None, :]) & (cp[:, None] < ws[None, :] + 16)
    cidx = np.clip(cp[:, None] - c[None, :] + 15, 0, 30)
    T = np.full((16, 2, 64, 7, 2, 64), NEG, np.float32)
    for bi in range(7):
        delta = 3 - bi
        for rp in range(2):
            for r in range(2):
                dr = 2 * delta + rp - r
                if -7 <= dr <= 7:
                    vals = rpb_l[:, dr + 7, :][:, cidx]
                    T[:, rp, :, bi, r, :] = np.where(colvalid[None], vals, np.float32(NEG))
    return np.ascontiguousarray(T.reshape(16, 128, 7 * 128))


def _row_masks(base, NP, rowoff=0):
    nrow = 2 * NP
    qa = np.full((32, nrow), NEG, np.float32)
    for rq in range(nrow):
        r = base + rq
        if 0 <= r < 128:
            rs = min(max(r - 4, 0), 120)
            lo, hi = rs, rs + 8
        else:
            lo, hi = r - 4, r + 4
        for rk in range(rq - 16, rq + 16):
            rr = base + rk
            if lo <= rr < hi:
                qa[(rk + rowoff) % 32, rq] = 0.0
    return np.repeat(qa, 64, axis=1).astype(ml_dtypes.bfloat16)


def _k_aug(NP):
    nrow = 2 * NP
    ka = np.zeros((32, nrow), np.float32)
    ka[np.arange(nrow) % 32, np.arange(nrow)] = 1.0
    return np.repeat(ka, 64, axis=1).astype(ml_dtypes.bfloat16)


def _layer_inputs(k, l, w_ada, b_ada, w_in, sgu_norm_g, sgu_norm_b, w_spatial, b_spatial, rpb, w_out, ln_g, ln_b):
    d = {}
    d[f"wss_{k}"] = _slabs(w_ada[l][:, 0:2 * D], 128)
    d[f"wg_{k}"] = _slabs(w_ada[l][:, 2 * D:3 * D], 512)
    d[f"bss_{k}"] = np.ascontiguousarray(b_ada[l][0:2 * D].reshape(32, 128).T)
    d[f"bg_{k}"] = np.ascontiguousarray(np.broadcast_to(b_ada[l][2 * D:3 * D], (128, D)))
    d[f"win_{k}"] = _slabs(w_in[l], 128)
    d[f"wout_{k}"] = _slabs(w_out[l], 512)
    d[f"wsT_{k}"] = np.ascontiguousarray(w_spatial[l].transpose(2, 0, 1).reshape(128, 1024))
    d[f"gam_{k}"] = np.ascontiguousarray(sgu_norm_g[l].reshape(8, 128).T)
    d[f"bet_{k}"] = np.ascontiguousarray(sgu_norm_b[l].reshape(1, 1024))
    d[f"bs_{k}"] = np.ascontiguousarray(b_spatial[l].reshape(1, 1024))
    d[f"bias_{k}"] = _bias_tables(rpb[l])
    d[f"lng_{k}"] = np.ascontiguousarray(np.broadcast_to(ln_g[l], (128, D)))
    d[f"lnb_{k}"] = np.ascontiguousarray(np.broadcast_to(ln_b[l], (128, D)))
    return d


def _window(xb, row0, nrows):
    rows = np.clip(np.arange(row0, row0 + nrows), 0, 127)
    return np.ascontiguousarray(xb.reshape(128, 64, D)[rows].reshape(nrows * 64, D))


_CACHE = {}


def _program(layer_ids, NP0, PQ, own):
    key = (tuple(layer_ids), NP0, PQ, own)
    if key not in _CACHE:
        _CACHE[key] = build_program(list(layer_ids), NP0, PQ, OWN=own)
    return _CACHE[key]


FUSED = False
PQ_UNIT = 4


def _launch(x_full, c, layer_ids, weights, own=OWN, placements=None):
    nL = len(layer_ids)
    NP0 = own + 4 * nL
    nc = _program(tuple(range(nL)), NP0, PQ_UNIT, own)
    consts = {
        "identb": np.eye(128, dtype=np.float32).astype(ml_dtypes.bfloat16),
        "identf": np.eye(128, dtype=np.float32),
        "ones_row": np.ones((1, 1024), np.float32),
        "kaug": _k_aug(NP0),
    }
    shared = dict(consts)
    for k, l in enumerate(layer_ids):
        shared.update(_layer_inputs(k, l, **weights))
    place = placements if placements is not None else [(core // 4, (core % 4) * own) for core in range(N_CORES)]
    in_maps = []
    for (b, prow0) in place:
        m = dict(shared)
        m["xw"] = _window(x_full[b], 2 * (prow0 - 2 * nL), 2 * NP0)
        m["c_r"] = np.ascontiguousarray(c[b].reshape(KC, 128).T)
        for k in range(nL):
            NPk = NP0 - 4 * k
            base = 2 * (prow0 - 2 * (nL - k))
            m[f"qaug_{k}"] = _row_masks(base, NPk, rowoff=4 * k)
        in_maps.append(m)
    res = run_bass_kernel_spmd(nc, in_maps, core_ids=list(range(len(place))))
    if placements is not None:
        return [res.results[i]["out"] for i in range(len(place))]
    out = np.empty((2, 8192, D), np.float32)
    for core, (b, prow0) in enumerate(place):
        out[b, prow0 * 128:(prow0 + own) * 128] = res.results[core]["out"]
    return out


def kernel(x, c, w_ada, b_ada, w_in, sgu_norm_g, sgu_norm_b, w_spatial, b_spatial, rpb, w_out, ln_g, ln_b):
    f = lambda a: np.ascontiguousarray(np.asarray(a, dtype=np.float32))
    weights = dict(w_ada=f(w_ada), b_ada=f(b_ada), w_in=f(w_in), sgu_norm_g=f(sgu_norm_g), sgu_norm_b=f(sgu_norm_b),
                   w_spatial=f(w_spatial), b_spatial=f(b_spatial), rpb=f(rpb), w_out=f(w_out), ln_g=f(ln_g), ln_b=f(ln_b))
    x = f(x)
    c = f(c)
    if FUSED:
        return _launch(x, c, [0, 1], weights)
    x1 = _launch(x, c, [0], weights)
    return _launch(x1, c, [1], weights)
```
